# Optimizing a Trainium2 kernel written in Bass

```python
import math
import jax, jax.numpy as jnp
from jax import lax
import numpy as np

D_MODEL = 1024
BATCH = 8
SEQ = 2048
DEPTH = 2

GRID_W = 64
CTX_LEN = 256
N_MIXERS = 2
N_HEADS = 8
KV_HEADS = 2
GROUP = N_HEADS // KV_HEADS
HEAD_DIM = D_MODEL // N_HEADS
Q_DIM = N_HEADS * HEAD_DIM
KV_DIM = KV_HEADS * HEAD_DIM
AXIS_DIM = HEAD_DIM // 2
AXIS_PAIRS = AXIS_DIM // 2
ROPE_THETA = 10000.0
Q_BLOCK = 128
ATTN_SCALE = 1.0 / math.sqrt(HEAD_DIM)
POOL_WINDOWS = (2, 4, 8, 16)
POOL_GROUPS = len(POOL_WINDOWS)
POOL_CH = D_MODEL // POOL_GROUPS
D_FF = 2816
CONV_W = 3
N_MOD = 6
EPS = 1e-6

kernel_name = "hybrid_attn_pool_convffn_diffusion_trunk"


def rmsnorm(x, g):
    xf = x.astype(jnp.float32)
    y = xf * lax.rsqrt(jnp.mean(xf * xf, axis=-1, keepdims=True) + EPS)
    return y.astype(x.dtype) * g


def modulate(h, shift, scale):
    return h * (1.0 + scale) + shift


def axial_rope_tables(rows, cols):
    r = jnp.repeat(jnp.arange(rows, dtype=jnp.float32), cols)
    cc = jnp.tile(jnp.arange(cols, dtype=jnp.float32), rows)
    inv = ROPE_THETA ** (-jnp.arange(AXIS_PAIRS, dtype=jnp.float32) / AXIS_PAIRS)
    ang = jnp.stack([r[:, None] * inv, cc[:, None] * inv], axis=1)
    return jnp.cos(ang), jnp.sin(ang)


def apply_axial_rope(x, cos, sin):
    B, L, H, _ = x.shape
    xa = x.reshape(B, L, H, 2, 2, AXIS_PAIRS)
    x1, x2 = xa[..., 0, :], xa[..., 1, :]
    c = cos[None, :, None]
    s = sin[None, :, None]
    out = jnp.stack([x1 * c - x2 * s, x2 * c + x1 * s], axis=-2)
    return out.reshape(B, L, H, HEAD_DIM).astype(x.dtype)


def qkv_proj(h, w_qkv, g_q, g_k):
    B, L, _ = h.shape
    qkv = h @ w_qkv
    q = qkv[..., :Q_DIM].reshape(B, L, N_HEADS, HEAD_DIM)
    k = qkv[..., Q_DIM:Q_DIM + KV_DIM].reshape(B, L, KV_HEADS, HEAD_DIM)
    v = qkv[..., Q_DIM + KV_DIM:].reshape(B, L, KV_HEADS, HEAD_DIM)
    return rmsnorm(q, g_q), rmsnorm(k, g_k), v


def kv_proj(h, w_qkv, g_k):
    B, L, _ = h.shape
    kv = h @ w_qkv[:, Q_DIM:]
    k = kv[..., :KV_DIM].reshape(B, L, KV_HEADS, HEAD_DIM)
    v = kv[..., KV_DIM:].reshape(B, L, KV_HEADS, HEAD_DIM)
    return rmsnorm(k, g_k), v


def gqa_block(qblk, k, v):
    s = jnp.einsum('bqkgd,bskd->bkgqs', qblk, k, preferred_element_type=jnp.float32) * ATTN_SCALE
    p = jax.nn.softmax(s, axis=-1).astype(v.dtype)
    return jnp.einsum('bkgqs,bskd->bqkgd', p, v)


def latent_attention(q, k_all, v_all):
    B, L, _, _ = q.shape
    nblk = L // Q_BLOCK
    qb = q.reshape(B, nblk, Q_BLOCK, KV_HEADS, GROUP, HEAD_DIM).transpose(1, 0, 2, 3, 4, 5)
    o = lax.map(lambda qblk: gqa_block(qblk, k_all, v_all), qb)
    return o.transpose(1, 0, 2, 3, 4, 5).reshape(B, L, Q_DIM)


def context_attention(q, k, v):
    B, L, _, _ = q.shape
    o = gqa_block(q.reshape(B, L, KV_HEADS, GROUP, HEAD_DIM), k, v)
    return o.reshape(B, L, Q_DIM)


def multiscale_pool_mix(h, w_pool, b_pool, pool_scale):
    B, L, _ = h.shape
    hg = h.reshape(B, L, POOL_GROUPS, POOL_CH)
    csum = jnp.concatenate(
        [jnp.zeros((B, 1, POOL_GROUPS, POOL_CH), jnp.float32),
         jnp.cumsum(hg.astype(jnp.float32), axis=1)], axis=1)
    t = jnp.arange(L)
    means = []
    for g, w in enumerate(POOL_WINDOWS):
        lo = jnp.clip(t - w // 2, 0, L)
        hi = jnp.clip(t + (w - w // 2), 0, L)
        cnt = (hi - lo).astype(jnp.float32)
        cg = csum[:, :, g]
        s = jnp.take(cg, hi, axis=1) - jnp.take(cg, lo, axis=1)
        means.append(s / cnt[None, :, None])
    pooled = jnp.stack(means, axis=2).astype(h.dtype) - hg
    y = jnp.einsum('blgc,gce->blge', pooled, w_pool) + b_pool
    return y.reshape(B, L, D_MODEL) * pool_scale


def conv_ffn(h, w_up, conv_w, conv_b, w_down):
    u = h @ w_up
    gate, val = u[..., :D_FF], u[..., D_FF:]
    gp = jnp.pad(gate, ((0, 0), (1, 1), (0, 0)))
    gate = gp[:, :-2] * conv_w[0] + gp[:, 1:-1] * conv_w[1] + gp[:, 2:] * conv_w[2] + conv_b
    return (jax.nn.silu(gate) * val) @ w_down


def setup_inputs(seed: int = 0) -> dict:
    key = jax.random.key(seed)
    ks = jax.random.split(key, 32)
    f32 = jnp.float32
    n_attn = (DEPTH + N_MIXERS - 1) // N_MIXERS
    n_pool = DEPTH // N_MIXERS
    nrm = lambda k, shape, s: jax.random.normal(k, shape, f32) * s
    gain = lambda k, shape: 1.0 + 0.05 * jax.random.normal(k, shape, f32)
    return {
        "x": nrm(ks[0], (BATCH, SEQ, D_MODEL), 1.0),
        "c": nrm(ks[1], (BATCH, D_MODEL), 1.0),
        "ctx": nrm(ks[2], (BATCH, CTX_LEN, D_MODEL), 1.0),
        "c_ctx": nrm(ks[3], (D_MODEL,), 1.0),
        "w_mod": nrm(ks[4], (DEPTH, D_MODEL, N_MOD * D_MODEL), 0.5 * D_MODEL ** -0.5),
        "b_mod": nrm(ks[5], (DEPTH, N_MOD * D_MODEL), 0.02),
        "g_pre_mix": gain(ks[6], (DEPTH, D_MODEL)),
        "g_post_mix": gain(ks[7], (DEPTH, D_MODEL)),
        "g_pre_ffn": gain(ks[8], (DEPTH, D_MODEL)),
        "g_post_ffn": gain(ks[9], (DEPTH, D_MODEL)),
        "w_qkv": nrm(ks[10], (n_attn, D_MODEL, Q_DIM + 2 * KV_DIM), D_MODEL ** -0.5),
        "g_q": gain(ks[11], (n_attn, HEAD_DIM)),
        "g_k": gain(ks[12], (n_attn, HEAD_DIM)),
        "w_o": nrm(ks[13], (n_attn, Q_DIM, D_MODEL), Q_DIM ** -0.5),
        "w_pool": nrm(ks[14], (n_pool, POOL_GROUPS, POOL_CH, POOL_CH), POOL_CH ** -0.5),
        "b_pool": nrm(ks[15], (n_pool, POOL_GROUPS, POOL_CH), 0.02),
        "pool_scale": gain(ks[16], (n_pool, D_MODEL)),
        "w_up": nrm(ks[17], (DEPTH, D_MODEL, 2 * D_FF), D_MODEL ** -0.5),
        "conv_w": nrm(ks[18], (DEPTH, CONV_W, D_FF), CONV_W ** -0.5),
        "conv_b": nrm(ks[19], (DEPTH, D_FF), 0.02),
        "w_down": nrm(ks[20], (DEPTH, D_FF, D_MODEL), D_FF ** -0.5),
    }


def reference(x, c, ctx, c_ctx, w_mod, b_mod, g_pre_mix, g_post_mix, g_pre_ffn, g_post_ffn,
              w_qkv, g_q, g_k, w_o, w_pool, b_pool, pool_scale, w_up, conv_w, conv_b, w_down):
    B, L, _ = x.shape
    ROWS = L // GRID_W
    cos, sin = axial_rope_tables(ROWS, GRID_W)
    sc = jax.nn.silu(c)
    sc_ctx = jax.nn.silu(c_ctx)
    for i in range(DEPTH):
        mixer = i % N_MIXERS
        j = i // N_MIXERS
        update_ctx = any(jj % N_MIXERS == 0 for jj in range(i + 1, DEPTH))
        m = (sc @ w_mod[i] + b_mod[i]).reshape(B, N_MOD, D_MODEL)[:, :, None, :]
        mc = (sc_ctx @ w_mod[i] + b_mod[i]).reshape(N_MOD, D_MODEL)
        shift_m, scale_m, gate_m, shift_f, scale_f, gate_f = [m[:, k] for k in range(N_MOD)]
        cshift_m, cscale_m, cgate_m, cshift_f, cscale_f, cgate_f = [mc[k] for k in range(N_MOD)]

        h = modulate(rmsnorm(x, g_pre_mix[i]), shift_m, scale_m)
        hc = modulate(rmsnorm(ctx, g_pre_mix[i]), cshift_m, cscale_m)
        if mixer == 0:
            q, k, v = qkv_proj(h, w_qkv[j], g_q[j], g_k[j])
            q = apply_axial_rope(q, cos, sin)
            k = apply_axial_rope(k, cos, sin)
            if update_ctx:
                qc, kc, vc = qkv_proj(hc, w_qkv[j], g_q[j], g_k[j])
            else:
                kc, vc = kv_proj(hc, w_qkv[j], g_k[j])
            k_all = jnp.concatenate([kc, k], axis=1)
            v_all = jnp.concatenate([vc, v], axis=1)
            y = latent_attention(q, k_all, v_all) @ w_o[j]
            if update_ctx:
                yc = context_attention(qc, kc, vc) @ w_o[j]
        else:
            y = multiscale_pool_mix(h, w_pool[j], b_pool[j], pool_scale[j])
            if update_ctx:
                yc = multiscale_pool_mix(hc, w_pool[j], b_pool[j], pool_scale[j])
        x = x + gate_m * rmsnorm(y, g_post_mix[i])
        if update_ctx:
            ctx = ctx + cgate_m * rmsnorm(yc, g_post_mix[i])

        h = modulate(rmsnorm(x, g_pre_ffn[i]), shift_f, scale_f)
        x = x + gate_f * rmsnorm(conv_ffn(h, w_up[i], conv_w[i], conv_b[i], w_down[i]), g_post_ffn[i])
        if update_ctx:
            hc = modulate(rmsnorm(ctx, g_pre_ffn[i]), cshift_f, cscale_f)
            ctx = ctx + cgate_f * rmsnorm(conv_ffn(hc, w_up[i], conv_w[i], conv_b[i], w_down[i]), g_post_ffn[i])
    return x
```

```python
import math
import numpy as np
from contextlib import ExitStack
import concourse.bass as bass
import concourse.mybir as mybir
from concourse.bass_utils import run_bass_kernel_spmd

F32 = mybir.dt.float32
BF16 = mybir.dt.bfloat16
AF = mybir.ActivationFunctionType
ALU = mybir.AluOpType
AX = mybir.AxisListType

L = 2048
D = 1024
NT = 16
DFF = 2816
NJ = 22
EPS = 1e-6
ATTN_SCALE = 1.0 / math.sqrt(128.0)
ENGS = ("pe", "act", "dve", "pool", "sp")
ARENA_BYTES = 131072


class Prog:
    def __init__(self, nc, es):
        self.nc = nc
        self.es = es
        self.sem = {e: es.enter_context(nc.semaphore("sem_" + e)) for e in ENGS}
        self.semobj = dict(self.sem)
        self.cnt = {e: 0 for e in ENGS}
        self.ops = {e: [] for e in ENGS}
        self.seen = {e: {} for e in ENGS}
        self.wrecs = []
        self.rrecs = []
        self.dcount = {}
        self.pend = {e: ([], []) for e in ENGS}

    def _deps(self, e, reads, writes):
        need = {}

        def add(k, v):
            if v > need.get(k, 0):
                need[k] = v
        for (sp, lo, hi) in reads:
            for w in self.wrecs:
                if w[0] == sp and w[1] < hi and lo < w[2]:
                    add(w[3], w[4])
            if sp == "P":
                for r in self.rrecs:
                    if r[0] == sp and r[1] < hi and lo < r[2] and r[3] != e:
                        add(r[3], r[4])
        for (sp, lo, hi) in writes:
            for w in self.wrecs:
                if w[0] == sp and w[1] < hi and lo < w[2]:
                    add(w[3], w[4])
            for r in self.rrecs:
                if r[0] == sp and r[1] < hi and lo < r[2] and r[3] != e:
                    add(r[3], r[4])
        waits = []
        for k, v in need.items():
            if k == "pe" and e == "pe":
                continue
            if self.seen[e].get(k, 0) >= v:
                continue
            self.seen[e][k] = v
            waits.append((k, v))
        return waits

    def _commit(self, key, val, reads, writes):
        for (sp, lo, hi) in writes:
            self.wrecs = [w for w in self.wrecs if not (w[0] == sp and lo <= w[1] and w[2] <= hi)]
            self.rrecs = [r for r in self.rrecs if not (r[0] == sp and lo <= r[1] and r[2] <= hi)]
            self.wrecs.append([sp, lo, hi, key, val])
        for (sp, lo, hi) in reads:
            self.rrecs = [r for r in self.rrecs
                          if not (r[0] == sp and r[3] == key and lo <= r[1] and r[2] <= hi)]
            self.rrecs.append([sp, lo, hi, key, val])

    def op(self, e, fn, reads=(), writes=(), inc=True):
        reads = list(reads)
        writes = list(writes)
        waits = self._deps(e, reads, writes)
        if inc:
            self.cnt[e] += 1
            pr, pw = self.pend[e]
            self._commit(e, self.cnt[e], pr + reads, pw + writes)
            self.pend[e] = ([], [])
            self.ops[e].append((waits, fn, (e, 1)))
        else:
            self.pend[e][0].extend(reads)
            self.pend[e][1].extend(writes)
            self.ops[e].append((waits, fn, None))

    def dma(self, q, semname, fn, reads=(), writes=()):
        key = "d_" + semname
        if key not in self.semobj:
            self.semobj[key] = self.es.enter_context(self.nc.semaphore(key))
            self.dcount[key] = 0
        reads = list(reads)
        writes = list(writes)
        waits = self._deps(q, reads, writes)
        self.dcount[key] += 16
        self._commit(key, self.dcount[key], reads, writes)
        self.ops[q].append((waits, fn, (key, 16)))

    def final_wait_all(self, e):
        waits = []
        for k in self.semobj:
            v = self.cnt[k] if k in ENGS else self.dcount[k]
            if v > 0 and k != e and self.seen[e].get(k, 0) < v:
                waits.append((k, v))
        self.ops[e].append((waits, None, None))

    def emit(self):
        nc = self.nc
        with nc.Block() as block:
            def mk(ename):
                def body(engine):
                    for waits, fn, inc in self.ops[ename]:
                        for (k, v) in waits:
                            engine.wait_ge(self.semobj[k], v)
                        if fn is None:
                            continue
                        inst = fn(engine)
                        if inc is not None:
                            inst.then_inc(self.semobj[inc[0]], inc[1])
                return body
            block.tensor(mk("pe"))
            block.scalar(mk("act"))
            block.vector(mk("dve"))
            block.gpsimd(mk("pool"))
            block.sync(mk("sp"))


class Buf:
    _vbase = [1 << 24]

    def __init__(self, ap, space, lo, nbytes, esize):
        self.ap = ap
        self.space = space
        self.lo = lo
        self.nbytes = nbytes
        self.esize = esize

    def __getitem__(self, idx):
        return self.ap[idx]

    def r(self, elo=None, ehi=None):
        if elo is None:
            lo, hi = self.lo, self.lo + self.nbytes
        else:
            lo, hi = self.lo + elo * self.esize, self.lo + ehi * self.esize
        if self.space == "P":
            lo = lo // 2048 * 2048
            hi = (hi + 2047) // 2048 * 2048
        return (self.space, lo, hi)


def esize_of(dt):
    return 2 if dt == BF16 else 4


def build_program(stop_after="full"):
    nc = bass.Bass("TRN2", target_bir_lowering=False)

    def din(name, shape):
        return nc.dram_tensor(name, list(shape), F32, kind="ExternalInput").ap()

    x_d = din("x", [L, D])
    ctx_d = din("ctx", [256, D])
    cc_d = din("cc", [16, 128])
    wmod_d = din("w_mod", [2, D, 6 * D])
    bmod_d = din("b_mod", [2, 6 * D])
    vecB_d = din("vecB", [76, 128])
    vecC_d = din("vecC", [132, 128])
    gpost_d = din("gpost", [4, D])
    vrows_d = din("vrows", [4, D])
    bpool_d = din("bpool", [1, D])
    wqkv_d = din("w_qkv", [D, 1536])
    wo_d = din("w_o", [D, D])
    wpool_d = din("w_pool", [4, 256, 256])
    wup_d = din("w_up", [2, D, 2 * DFF])
    wdown_d = din("w_down", [2, DFF, D])
    cos_d = din("rope_cos", [128, NT, 128])
    sin_d = din("rope_sin", [128, NT, 128])
    poolM_d = din("poolM", [20, 128, 128])
    y_d = nc.dram_tensor("y", [L, D], F32, kind="ExternalOutput").ap()

    es = ExitStack()
    with es:
        P = Prog(nc, es)
        vb = [1 << 24]

        def sb(name, shape, dt):
            t = es.enter_context(nc.sbuf_tensor(name, list(shape), dt))
            n = int(np.prod(shape[1:])) * esize_of(dt)
            b = Buf(t, "S", vb[0], n, esize_of(dt))
            vb[0] += (n + 1023) // 1024 * 1024 + 1024
            return b

        arena_t = es.enter_context(nc.sbuf_tensor("arena", [128, ARENA_BYTES // 4], F32))
        psum_t = es.enter_context(nc.psum_tensor("psum", [128, 4096], F32))

        def ar(off, shape, dt, parts=128):
            n = int(np.prod(shape)) * esize_of(dt)
            assert off % 4 == 0 and off + n <= ARENA_BYTES, (off, n)
            ap = arena_t[0:parts, off // 4:(off + n + 3) // 4]
            if dt == BF16:
                ap = ap.bitcast(BF16)
            ap = ap[:, 0:int(np.prod(shape))]
            if len(shape) == 2:
                ap = ap.rearrange("p (a b) -> p a b", a=shape[0])
            elif len(shape) == 3:
                ap = ap.rearrange("p (a b c) -> p a b c", a=shape[0], b=shape[1])
            elif len(shape) == 4:
                ap = ap.rearrange("p (a b c d) -> p a b c d", a=shape[0], b=shape[1], c=shape[2])
            return Buf(ap, "S", off, n, esize_of(dt))

        def pbank(b, nb=1, dt=F32):
            ap = psum_t[:, b * 512:(b + nb) * 512]
            if dt == BF16:
                ap = ap.bitcast(BF16)
            return Buf(ap, "P", b * 2048, nb * 2048, esize_of(dt))

        xs = sb("xs", [128, NT, D], F32)
        ident = sb("ident", [128, 128], BF16)
        identf = sb("identf", [128, 128], F32)
        onesb = sb("onesb", [128, 128], BF16)
        sel = sb("sel", [128, 4, 128], F32)
        vecT_B = sb("vecT_B", [128, 76], F32)
        vecT_C = sb("vecT_C", [128, 132], F32)
        modT = sb("modT", [128, 2, 6, 8, 2], F32)
        AmT = sb("AmT", [128, 2, 8], F32)
        AfT = sb("AfT", [128, 2, 8], F32)
        AcT = sb("AcT", [128, 8], F32)
        Grows = sb("Grows", [128, D], F32)
        Gbc = sb("Gbc", [128, D], F32)
        stat = sb("stat", [128, 64], F32)
        halo = sb("halo", [128, 128], F32)
        gqk = sb("gqk", [128, 2, 128], F32)

        stat_i = [0]

        def stat_cols(n):
            if stat_i[0] + n > 64:
                stat_i[0] = 0
            a = stat_i[0]
            stat_i[0] += n
            return a

        def xr(t):
            return xs.r(t * D, (t + 1) * D)

        def mm(out, lhsT, rhs, start, stop, reads, writes, inc):
            P.op("pe", lambda e: e.matmul(out, lhsT=lhsT, rhs=rhs, start=start, stop=stop),
                 reads=reads, writes=writes, inc=inc)

        def tr(out, in_, idn, reads, writes, inc):
            P.op("pe", lambda e: e.transpose(out=out, in_=in_, identity=idn),
                 reads=reads, writes=writes, inc=inc)

        def act(out, in_, func, reads, writes, **kw):
            P.op("act", lambda e: e.activation(out=out, in_=in_, func=func, **kw),
                 reads=reads, writes=writes)

        def tt(eng, out, in0, in1, op, reads, writes):
            P.op(eng, lambda e: e.tensor_tensor(out=out, in0=in0, in1=in1, op=op),
                 reads=reads, writes=writes)

        def ts(eng, out, in0, s1, s2, op0, op1, reads, writes):
            if s2 is None:
                P.op(eng, lambda e: e.tensor_scalar(out=out, in0=in0, scalar1=s1, scalar2=None, op0=op0),
                     reads=reads, writes=writes)
            else:
                P.op(eng, lambda e: e.tensor_scalar(out=out, in0=in0, scalar1=s1, scalar2=s2, op0=op0, op1=op1),
                     reads=reads, writes=writes)

        def stt(out, in0, scalar, in1, op0, op1, reads, writes):
            P.op("dve", lambda e: e.scalar_tensor_tensor(out=out, in0=in0, scalar=scalar, in1=in1, op0=op0, op1=op1),
                 reads=reads, writes=writes)

        def cp(eng, out, in_, reads, writes):
            if eng == "act":
                act(out, in_, AF.Copy, reads, writes)
            else:
                P.op(eng, lambda e: e.tensor_copy(out=out, in_=in_), reads=reads, writes=writes)

        def recip(out, in_, reads, writes):
            P.op("dve", lambda e: e.reciprocal(out=out, in_=in_), reads=reads, writes=writes)

        def memset(eng, buf_ap, val, writes):
            P.op(eng, lambda e: e.memset(buf_ap, val), writes=writes)

        def rstd_from_ss(ss_ap, ss_rng, n_inv):
            act(ss_ap, ss_ap, AF.Sqrt, [ss_rng], [ss_rng], scale=n_inv, bias=EPS)
            recip(ss_ap, ss_ap, [ss_rng], [ss_rng])

        def bcast_row(rows, i, c0, n, pb):
            mm(pb[:, 0:n], sel[:, i, :], rows[:, c0:c0 + n], True, True,
               [sel.r(), rows.r()], [pb.r()], True)

        for t in range(NT):
            P.dma("sp", "x%d" % t, lambda e, t=t: e.dma_start(out=xs[:, t, :], in_=x_d[t * 128:(t + 1) * 128, :]),
                  writes=[xr(t)])

        memset("pool", identf[:], 0.0, [identf.r()])
        P.op("pool", lambda e: e.affine_select(out=identf[:], in_=identf[:], pattern=[[-1, 128]],
                                               compare_op=ALU.not_equal, fill=1.0, base=0, channel_multiplier=1),
             reads=[identf.r()], writes=[identf.r()])
        cp("dve", ident[:], identf[:], [identf.r()], [ident.r()])
        memset("dve", onesb[:], 1.0, [onesb.r()])
        memset("pool", sel[:], 0.0, [sel.r()])
        P.op("pool", lambda e: e.affine_select(out=sel[:], in_=sel[:], pattern=[[-1, 4], [0, 128]],
                                               compare_op=ALU.not_equal, fill=1.0, base=0, channel_multiplier=1),
             reads=[sel.r()], writes=[sel.r()])
        memset("dve", Grows[:], 0.0, [Grows.r()])

        wm = [ar(0, [8, 1024], BF16), ar(16384, [8, 1024], BF16)]
        bmod = ar(32768, [6144], BF16, parts=33)
        stageA = ar(45056, [128], F32, parts=16)
        stageB = ar(45568, [128], F32, parts=76)
        stageC0 = ar(46080, [128], F32, parts=66)
        stageC1 = ar(46592, [128], F32, parts=66)
        Sb = ar(47104, [16], BF16)
        S4 = ar(47168, [8, 4, 4], BF16)
        Eb = ar(47424, [4, 4], BF16, parts=33)
        gpost = ar(47488, [D], F32, parts=4)
        Vrows = ar(51584, [D], F32)
        cosF = ar(55680, [NT, 128], F32)
        sinF = ar(63872, [NT, 128], F32)
        ropeT = ar(75776, [4, NT, 128], BF16)

        P.dma("sp", "stA", lambda e: e.dma_start(out=stageA[:], in_=cc_d[:, :]), writes=[stageA.r()])
        P.dma("sp", "stB", lambda e: e.dma_start(out=stageB[:], in_=vecB_d[:, :]), writes=[stageB.r()])
        P.dma("sp", "stC0", lambda e: e.dma_start(out=stageC0[:], in_=vecC_d[0:66, :]), writes=[stageC0.r()])
        P.dma("sp", "stC1", lambda e: e.dma_start(out=stageC1[:], in_=vecC_d[66:132, :]), writes=[stageC1.r()])
        P.dma("sp", "gpost", lambda e: e.dma_start(out=gpost[:], in_=gpost_d[:, :]), writes=[gpost.r()])
        memset("dve", Vrows[:], 0.0, [Vrows.r()])
        P.dma("sp", "vrows", lambda e: e.dma_start(out=Vrows[0:4, :], in_=vrows_d[:, :]), writes=[Vrows.r()])
        P.dma("sp", "cosF", lambda e: e.dma_start(out=cosF[:], in_=cos_d[:, :, :]), writes=[cosF.r()])
        P.dma("sp", "sinF", lambda e: e.dma_start(out=sinF[:], in_=sin_d[:, :, :]), writes=[sinF.r()])
        for l in range(2):
            P.dma("pool", "bmod%d" % l, lambda e, l=l: e.dma_start(out=bmod[l * 32:l * 32 + 1, :].rearrange("o (a b) -> o a b", a=6),
                                                                   in_=bmod_d[l:l + 1, :].rearrange("o (a b) -> o a b", a=6)),
                  writes=[bmod.r()])

        pb0 = pbank(0)
        act(stageA[:], stageA[:], AF.Silu, [stageA.r()], [stageA.r()])
        tr(pb0[:, 0:16], stageA[:], identf[0:16, 0:16], [stageA.r(), identf.r()], [pb0.r()], True)
        cp("dve", Sb[:], pb0[:, 0:16], [pb0.r()], [Sb.r()])
        memset("dve", S4[:], 0.0, [S4.r()])
        for i in range(4):
            cp("dve", S4[:, :, i, i], Sb[:, 0:8], [Sb.r()], [S4.r()])
        memset("pool", Eb[:], 0.0, [Eb.r()])
        P.op("pool", lambda e: e.affine_select(out=Eb[:], in_=Eb[:], pattern=[[1, 4], [-1, 4]],
                                               compare_op=ALU.not_equal, fill=1.0, base=0, channel_multiplier=0),
             reads=[Eb.r()], writes=[Eb.r()])
        pb1 = pbank(1)
        tr(pb1[:, 0:76], stageB[:], identf[0:76, 0:76], [stageB.r(), identf.r()], [pb1.r()], True)
        cp("dve", vecT_B[:], pb1[:, 0:76], [pb1.r()], [vecT_B.r()])
        pb2 = pbank(2)
        tr(pb2[:, 0:66], stageC0[:], identf[0:66, 0:66], [stageC0.r(), identf.r()], [pb2.r()], False)
        tr(pb2[:, 66:132], stageC1[:], identf[0:66, 0:66], [stageC1.r(), identf.r()], [pb2.r()], True)
        cp("dve", vecT_C[:], pb2[:, 0:132], [pb2.r()], [vecT_C.r()])

        pb3 = pbank(3)
        for i in range(2):
            bcast_row(Vrows, i, 0, 128, pb3)
            cp("dve", gqk[:, i, :], pb3[:, 0:128], [pb3.r()], [gqk.r()])
        for i in range(2):
            g = gqk[:, i, :]
            tt("dve", ropeT[:, 2 * i, :, :], cosF[:], g.unsqueeze(1).to_broadcast([128, NT, 128]), ALU.mult,
               [cosF.r(), gqk.r()], [ropeT.r()])
            gv = g.rearrange("p (a h i) -> p a h i", a=2, h=2)
            Bv = ropeT[:, 2 * i + 1, :, :].rearrange("p t (a h i) -> p t a h i", a=2, h=2)
            Sv = sinF[:].rearrange("p t (a h i) -> p t a h i", a=2, h=2)
            for hh in range(2):
                tt("dve", Bv[:, :, :, hh, :], Sv[:, :, :, hh, :],
                   gv[:, :, 1 - hh, :].unsqueeze(1).to_broadcast([128, NT, 2, 32]), ALU.mult,
                   [sinF.r(), gqk.r()], [ropeT.r()])

        mps = pbank(4)
        memset("dve", mps[:, 0:192], 0.0, [mps.r()])
        mpsv = mps[:, 0:192].rearrange("p (l b n t) -> p l b n t", l=2, b=6, n=8)
        gps = pbank(5, 2)
        S2 = Sb[:].rearrange("p (t k) -> p k t", t=2)
        first_g = [True, True]
        nblk = 0
        for l in range(2):
            for blk in range(6):
                w = wm[nblk % 2]
                nblk += 1
                P.dma("pool", "wm%d" % (nblk % 2),
                      lambda e, w=w, l=l, blk=blk: e.dma_start(
                          out=w[:], in_=wmod_d[l, :, blk * 1024:(blk + 1) * 1024].rearrange("(c p) n -> p c n", p=128)),
                      writes=[w.r()])
                if blk in (2, 5):
                    i = 2 * l + (0 if blk == 2 else 1)
                    last = (l == 1 and blk == 5)
                    for half in range(2):
                        o = gps[0:4, half * 512:(half + 1) * 512]
                        for kc in range(8):
                            mm(o, S4[:, kc, i, :], w[:, kc, half * 512:(half + 1) * 512], first_g[half], False,
                               [S4.r(), w.r()], [gps.r()], False)
                            first_g[half] = False
                        mm(o, Eb[l * 32:l * 32 + 1, i, :],
                           bmod[l * 32:l * 32 + 1, blk * 1024 + half * 512: blk * 1024 + (half + 1) * 512],
                           False, last, [Eb.r(), bmod.r()], [gps.r()], half == 1)
                else:
                    for n in range(8):
                        o = mpsv[:, l, blk, n, :]
                        for kc in range(8):
                            mm(o, w[:, kc, n * 128:(n + 1) * 128], S2[:, kc, :], kc == 0, False,
                               [w.r(), Sb.r()], [mps.r()], False)
                        mm(o, bmod[l * 32:l * 32 + 1, blk * 1024 + n * 128: blk * 1024 + (n + 1) * 128],
                           onesb[l * 32:l * 32 + 1, 0:2], False, True, [bmod.r(), onesb.r()], [mps.r()], n == 7)
        cp("dve", modT[:].rearrange("p l b n t -> p (l b n t)"), mps[:, 0:192], [mps.r()], [modT.r()])
        cp("dve", Grows[0:4, :], gps[0:4, :], [gps.r()], [Grows.r()])
        tt("dve", Grows[0:4, :], Grows[0:4, :], gpost[:], ALU.mult, [Grows.r(), gpost.r()], [Grows.r()])
        for l in range(2):
            stt(AmT[:, l, :], modT[:, l, 1, :, 0], 1.0, vecT_B[:, l * 8:(l + 1) * 8], ALU.add, ALU.mult,
                [modT.r(), vecT_B.r()], [AmT.r()])
            stt(AfT[:, l, :], modT[:, l, 4, :, 0], 1.0, vecT_B[:, 16 + l * 8:16 + (l + 1) * 8], ALU.add, ALU.mult,
                [modT.r(), vecT_B.r()], [AfT.r()])
        stt(AcT[:], modT[:, 0, 1, :, 1], 1.0, vecT_B[:, 0:8], ALU.add, ALU.mult, [modT.r(), vecT_B.r()], [AcT.r()])

        def finish():
            if stop_after != "full":
                for t in range(NT):
                    P.dma("sp", "y%d" % (t % 4), lambda e, t=t: e.dma_start(out=y_d[t * 128:(t + 1) * 128, :], in_=xs[:, t, :]),
                          reads=[xr(t)])
            P.final_wait_all("sp")
            P.emit()

        if stop_after == "p0":
            finish()
            return nc

        def load_gbc(i):
            for half in range(2):
                pb = pbank(6 + half)
                bcast_row(Grows, i, half * 512, 512, pb)
                cp("act", Gbc[:, half * 512:(half + 1) * 512], pb[:, 0:512], [pb.r()],
                   [Gbc.r(half * 512, (half + 1) * 512)])

        def norm_tile(src_ap, src_rng, xn, junk):
            c = stat_cols(1)
            ssr = stat.r(c, c + 1)
            act(junk[:], src_ap, AF.Square, [src_rng], [junk.r(), ssr], accum_out=stat[:, c:c + 1])
            rstd_from_ss(stat[:, c:c + 1], ssr, 1.0 / D)
            act(xn[:], src_ap, AF.Copy, [src_rng, ssr], [xn.r()], scale=stat[:, c:c + 1])

        tm_i = [0]

        def transpose_mod(xn, tpb, dst_fn, A_ap, S_ap_fn, A_rng, S_rng):
            tpv = tpb[:, 0:1024].rearrange("p (c n) -> p c n", c=8)
            for c in range(8):
                tr(tpv[:, c, :], xn[:, c * 128:(c + 1) * 128], ident[:], [xn.r(), ident.r()], [tpb.r()], c == 7)
            tm_i[0] += 1
            for c in range(8):
                dst, drng = dst_fn(c)
                if tm_i[0] % 2 == 0:
                    ts("dve", dst, tpv[:, c, :], A_ap[:, c:c + 1], S_ap_fn(c), ALU.mult, ALU.add,
                       [tpb.r(), A_rng, S_rng], [drng])
                else:
                    act(dst, tpv[:, c, :], AF.Identity, [tpb.r(), A_rng, S_rng], [drng],
                        scale=A_ap[:, c:c + 1], bias=S_ap_fn(c))

        def post_tile(ypb, t, yg, final):
            c = stat_cols(1)
            ssr = stat.r(c, c + 1)
            act(yg[:], ypb[:, 0:1024], AF.Square, [ypb.r()], [yg.r(), ssr], accum_out=stat[:, c:c + 1])
            rstd_from_ss(stat[:, c:c + 1], ssr, 1.0 / D)
            tt("dve", yg[:], ypb[:, 0:1024], Gbc[:], ALU.mult, [ypb.r(), Gbc.r()], [yg.r()])
            stt(xs[:, t, :], yg[:], stat[:, c:c + 1], xs[:, t, :], ALU.mult, ALU.add,
                [yg.r(), ssr, xr(t)], [xr(t)])
            if final:
                P.dma("sp", "y%d" % (t % 4), lambda e, t=t: e.dma_start(out=y_d[t * 128:(t + 1) * 128, :], in_=xs[:, t, :]),
                      reads=[xr(t)])

        QT = ar(0, [8, L], BF16)
        KT = ar(32768, [2, 2304], BF16)
        Vs = ar(41984, [18, 256], BF16)
        wqkv = ar(51200, [8, 1536], BF16)
        tmps = []
        for s in range(2):
            b0 = 92160 + s * 15360
            tmps.append(dict(sq=ar(b0, [1280], F32), qn=ar(b0 + 5120, [1280], BF16), t1=ar(b0 + 7680, [1280], BF16),
                             t2=ar(b0 + 10240, [1280], BF16), qr=ar(b0 + 12800, [1280], BF16)))
        xnA = [ar(122880 + s * 2048, [D], BF16) for s in range(2)]
        hTt = [ar(126976 + s * 2048, [8, 128], BF16) for s in range(2)]
        ctxs = [ar(s * 4096, [D], F32) for s in range(2)]

        for k4 in range(4):
            P.dma("pool", "wqkv%d" % k4,
                  lambda e, k4=k4: e.dma_start(out=wqkv[:, 2 * k4:2 * k4 + 2, :],
                                               in_=wqkv_d[k4 * 256:(k4 + 1) * 256, :].rearrange("(c p) n -> p c n", p=128)),
                  writes=[wqkv.r(2 * k4 * 1536, (2 * k4 + 2) * 1536)])
        for s in range(2):
            P.dma("sp", "ctx%d" % s, lambda e, s=s: e.dma_start(out=ctxs[s][:], in_=ctx_d[s * 128:(s + 1) * 128, :]),
                  writes=[ctxs[s].r()])

        tiles = [("c", 0), ("c", 1)] + [("x", t) for t in range(NT)]
        for it, (kind, t) in enumerate(tiles):
            s = it % 2
            xn = xnA[s]
            hT = hTt[s]
            tm = tmps[s]
            if kind == "c":
                src_ap, src_rng = ctxs[t][:], ctxs[t].r()
                A_ap, A_rng = AcT, AcT.r()
                S_fn = lambda c: modT[:, 0, 0, c, 1:2]
                kc = t
            else:
                src_ap, src_rng = xs[:, t, :], xr(t)
                A_ap, A_rng = AmT[:, 0, :], AmT.r()
                S_fn = lambda c: modT[:, 0, 0, c, 0:1]
                kc = 2 + t
            junkA = Buf(tm["t2"][:, 0:D], "S", tm["t2"].lo, D * 2, 2)
            norm_tile(src_ap, src_rng, xn, junkA)
            tpb = pbank(3 + s, 1, BF16)
            transpose_mod(xn, tpb, lambda c, hT=hT: (hT[:, c, :], hT.r(c * 128, (c + 1) * 128)),
                          A_ap, S_fn, A_rng, modT.r())
            qkv = pbank(0, 3)
            if kind == "x":
                blocks = [(0, 0), (1, 512), (2, 1024)]
            else:
                blocks = [(2, 1024)]
            for (bi, c0) in blocks:
                for k in range(8):
                    mm(qkv[:, bi * 512:(bi + 1) * 512], hT[:, k, :], wqkv[:, k, c0:c0 + 512], k == 0, k == 7,
                       [hT.r(), wqkv.r()], [qkv.r(bi * 512, (bi + 1) * 512)], k == 7)
            if stop_after == "qkvA":
                finish()
                return nc
            c_lo, nh = (0, 10) if kind == "x" else (1024, 2)
            ncol = nh * 128
            src = qkv[:, c_lo:c_lo + ncol]
            srng = qkv.r(c_lo, c_lo + ncol)
            sq, qn, t1, t2, qr = tm["sq"], tm["qn"], tm["t1"], tm["t2"], tm["qr"]
            act(sq[:, 0:ncol], src, AF.Square, [srng], [sq.r()])
            c = stat_cols(10)
            ssr = stat.r(c, c + 10)
            P.op("dve", lambda e, c=c, nh=nh, sq=sq, ncol=ncol: e.tensor_reduce(
                out=stat[:, c:c + nh], in_=sq[:, 0:ncol].rearrange("p (h d) -> p h d", h=nh), axis=AX.X, op=ALU.add),
                reads=[sq.r()], writes=[ssr])
            rstd_from_ss(stat[:, c:c + nh], ssr, 1.0 / 128)
            tt("dve", qn[:, 0:ncol].rearrange("p (h d) -> p h d", h=nh), src.rearrange("p (h d) -> p h d", h=nh),
               stat[:, c:c + nh].unsqueeze(2).to_broadcast([128, nh, 128]), ALU.mult, [srng, ssr], [qn.r()])
            if kind == "x":
                for (o, n_h, ti) in ((0, 8, 0), (1024, 2, 2)):
                    qv = qn[:, o:o + n_h * 128].rearrange("p (h d) -> p h d", h=n_h)
                    tt("dve", t1[:, o:o + n_h * 128].rearrange("p (h d) -> p h d", h=n_h), qv,
                       ropeT[:, ti, t, :].unsqueeze(1).to_broadcast([128, n_h, 128]), ALU.mult,
                       [qn.r(), ropeT.r()], [t1.r()])
                    q5 = qn[:, o:o + n_h * 128].rearrange("p (n h i) -> p n h i", h=2, i=32)
                    t5 = t2[:, o:o + n_h * 128].rearrange("p (n h i) -> p n h i", h=2, i=32)
                    Bt = ropeT[:, ti + 1, t, :].rearrange("p (a h i) -> p a h i", a=2, h=2)
                    for hh in range(2):
                        o5 = t5[:, :, hh, :].rearrange("p (hd a) i -> p hd a i", a=2)
                        i5 = q5[:, :, 1 - hh, :].rearrange("p (hd a) i -> p hd a i", a=2)
                        tt("dve", o5, i5, Bt[:, :, hh, :].unsqueeze(1).to_broadcast([128, n_h, 2, 32]), ALU.mult,
                           [qn.r(), ropeT.r()], [t2.r()])
                tt("dve", qr[:], t1[:], t2[:], ALU.add, [t1.r(), t2.r()], [qr.r()])
            else:
                tt("dve", qr[:, 0:256].rearrange("p (h d) -> p h d", h=2), qn[:, 0:256].rearrange("p (h d) -> p h d", h=2),
                   gqk[:, 1, :].unsqueeze(1).to_broadcast([128, 2, 128]), ALU.mult, [qn.r(), gqk.r()], [qr.r()])
            if stop_after == "qkvB":
                finish()
                return nc
            if kind == "x":
                tq = pbank(5, 1, BF16)
                tqv = tq[:, 0:1024].rearrange("p (h n) -> p h n", h=8)
                for h in range(8):
                    tr(tqv[:, h, :], qr[:, h * 128:(h + 1) * 128], ident[:], [qr.r(), ident.r()], [tq.r()], h == 7)
                cp("act", QT[:, :, t * 128:(t + 1) * 128], tqv, [tq.r()], [QT.r(t * 128, 7 * L + (t + 1) * 128)])
                koff = 1024
            else:
                koff = 0
            tk = pbank(6, 1, BF16)
            tkv = tk[:, 0:256].rearrange("p (h n) -> p h n", h=2)
            for g in range(2):
                tr(tkv[:, g, :], qr[:, koff + g * 128:koff + (g + 1) * 128], ident[:], [qr.r(), ident.r()], [tk.r()], g == 1)
            cp("act", KT[:, :, kc * 128:(kc + 1) * 128], tkv, [tk.r()], [KT.r(kc * 128, 2304 + (kc + 1) * 128)])
            cp("act", Vs[:, kc, :], qkv[:, 1280:1536], [qkv.r(1280, 1536)], [Vs.r(kc * 256, (kc + 1) * 256)])
            if stop_after == "qkv1" or (stop_after == "qkv3" and it == 2):
                finish()
                return nc

        if stop_after == "qkv":
            finish()
            return nc
        OT = ar(51200, [8, L], BF16)
        wo = ar(83968, [8, D], BF16)
        PT = [ar(100352 + s * 1024, [512], BF16) for s in range(3)]
        rden = [ar(103424 + s * 2048, [512], F32) for s in range(2)]
        ygA = [ar(107520 + s * 4096, [D], F32) for s in range(2)]
        NK = 18
        it2 = 0
        for h in range(8):
            g = h // 4
            for qb in range(4):
                qsl = slice(qb * 512, (qb + 1) * 512)
                ob = pbank(3 + (it2 % 2))
                db = pbank(5 + (it2 % 2))

                def s_mm(kc, h=h, g=g, qsl=qsl, qb=qb):
                    sbk = pbank(kc % 3)
                    mm(sbk[:, 0:512], KT[:, g, kc * 128:(kc + 1) * 128], QT[:, h, qsl], True, True,
                       [KT.r(g * 2304 + kc * 128, g * 2304 + (kc + 1) * 128), QT.r(h * L + qb * 512, h * L + (qb + 1) * 512)],
                       [sbk.r()], True)
                s_mm(0)
                s_mm(1)
                for kc in range(NK):
                    sbk = pbank(kc % 3)
                    pt = PT[kc % 3]
                    act(pt[:], sbk[:, 0:512], AF.Exp, [sbk.r()], [pt.r()], scale=ATTN_SCALE)
                    if kc + 2 < NK:
                        s_mm(kc + 2)
                    mm(ob[:, 0:512], Vs[:, kc, g * 128:(g + 1) * 128], pt[:], kc == 0, kc == NK - 1,
                       [Vs.r(kc * 256, (kc + 1) * 256), pt.r()], [ob.r()], kc == NK - 1)
                    mm(db[:, 0:512], onesb[:], pt[:], kc == 0, kc == NK - 1,
                       [onesb.r(), pt.r()], [db.r()], True)
                rd = rden[it2 % 2]
                recip(rd[:], db[:, 0:512], [db.r()], [rd.r()])
                tt("dve", OT[:, h, qsl], ob[:, 0:512], rd[:], ALU.mult, [ob.r(), rd.r()],
                   [OT.r(h * L + qb * 512, h * L + (qb + 1) * 512)])
                it2 += 1
                if h == 0 and qb == 1:
                    for k4 in range(4):
                        P.dma("pool", "wo%d" % k4,
                              lambda e, k4=k4: e.dma_start(out=wo[:, 2 * k4:2 * k4 + 2, :],
                                                           in_=wo_d[k4 * 256:(k4 + 1) * 256, :].rearrange("(c p) n -> p c n", p=128)),
                              writes=[wo.r(2 * k4 * D, (2 * k4 + 2) * D)])

        load_gbc(0)
        for t in range(NT):
            yb = pbank((t % 3) * 2, 2)
            for hh in range(8):
                for dh in range(2):
                    mm(yb[:, dh * 512:(dh + 1) * 512], OT[:, hh, t * 128:(t + 1) * 128], wo[:, hh, dh * 512:(dh + 1) * 512],
                       hh == 0, hh == 7, [OT.r(hh * L + t * 128, hh * L + (t + 1) * 128), wo.r()],
                       [yb.r(dh * 512, (dh + 1) * 512)], hh == 7)
            post_tile(yb, t, ygA[t % 2], False)

        hTf = ar(0, [8, 1152], BF16)
        wub = [ar(18432 + s * 4096, [8, 256], BF16) for s in range(3)]
        aT = ar(30720, [NJ, 1024], BF16)
        Tb = [ar(75776 + s * 4096, [1024], F32) for s in range(2)]
        Sbuf = [ar(83968, [1024], BF16)]
        wd = ar(86016, [NJ, D], BF16)
        xnF = [ar(30720 + s * 2048, [D], BF16) for s in range(2)]
        junkF = ar(34816, [D], BF16)
        ygF = [ar(75776 + s * 4096, [D], F32) for s in range(2)]

        def ffn(l, final):
            def load_wu(j, slot):
                w = wub[slot]
                for part in range(2):
                    c0 = part * DFF + j * 128
                    P.dma("pool", "wu%d_%d" % (slot, part),
                          lambda e, w=w, part=part, c0=c0: e.dma_start(
                              out=w[:, :, part * 128:(part + 1) * 128],
                              in_=wup_d[l, :, c0:c0 + 128].rearrange("(c p) n -> p c n", p=128)),
                          writes=[w.r()])

            def load_wd(g2):
                P.dma("pool", "wd%d" % g2,
                      lambda e, g2=g2: e.dma_start(out=wd[:, 2 * g2:2 * g2 + 2, :],
                                                   in_=wdown_d[l, g2 * 256:(g2 + 1) * 256, :].rearrange("(c p) n -> p c n", p=128)),
                      writes=[wd.r(2 * g2 * D, (2 * g2 + 2) * D)])

            load_gbc(2 * l + 1)
            for half in range(2):
                t0 = 0 if half == 0 else 8
                mlo = 0
                hcol = 1024
                load_wu(0, 0)
                load_wu(1, 1)
                for i in range(9 if half == 0 else 8):
                    t = t0 + i
                    xn = xnF[i % 2]
                    norm_tile(xs[:, t, :], xr(t), xn, junkF)
                    tpb = pbank(4 + (i % 2), 1, BF16)
                    transpose_mod(xn, tpb,
                                  lambda c, i=i: (hTf[:, c, i * 128:(i + 1) * 128], hTf.r(c * 1152 + i * 128, c * 1152 + (i + 1) * 128)),
                                  AfT[:, l, :], lambda c: modT[:, l, 3, c, 0:1], AfT.r(), modT.r())
                hps = pbank(7)
                vi = 0
                for j in range(NJ):
                    jj = 0
                    w = wub[j % 3]
                    if j + 2 < NJ:
                        load_wu(j + 2, (j + 2) % 3)
                    if half == 0 and j % 2 == 0:
                        load_wd(j // 2)
                    gp = pbank(2 * (j % 2), 2)
                    for tb in range(2):
                        for k in range(8):
                            mm(gp[:, tb * 512:(tb + 1) * 512], w[:, k, jj * 128:(jj + 1) * 128],
                               hTf[:, k, mlo + tb * 512: mlo + (tb + 1) * 512], k == 0, k == 7,
                               [w.r(), hTf.r()], [gp.r(tb * 512, (tb + 1) * 512)], k == 7)
                    hc = half * 64 + 2 * j
                    if half == 0:
                        for k in range(8):
                            mm(hps[:, hc:hc + 1], w[:, k, jj * 128:(jj + 1) * 128], hTf[:, k, hcol:hcol + 1], k == 0, k == 7,
                               [w.r(), hTf.r()], [hps.r(hc, hc + 1)], k == 7)
                        cp("dve", halo[:, hc:hc + 1], hps[:, hc:hc + 1], [hps.r()], [halo.r(hc, hc + 1)])
                        cp("dve", halo[:, 64 + hc:64 + hc + 1], gp[:, 1023:1024], [gp.r()], [halo.r(64 + hc, 64 + hc + 1)])
                    vps = []
                    for tb in range(2):
                        vp = pbank(4 + (vi % 3))
                        vi += 1
                        vps.append(vp)
                        for k in range(8):
                            mm(vp[:, 0:512], w[:, k, 128:256],
                               hTf[:, k, mlo + tb * 512: mlo + (tb + 1) * 512], k == 0, k == 7,
                               [w.r(), hTf.r()], [vp.r()], k == 7)
                    T = Tb[j % 2]
                    S = Sbuf[0]
                    cw = lambda tap: vecT_C[:, l * 66 + tap * 22 + j: l * 66 + tap * 22 + j + 1]
                    cb = vecT_B[:, 32 + l * 22 + j: 32 + l * 22 + j + 1]
                    act(T[:], gp[:, 0:1024], AF.Identity, [gp.r(), vecT_C.r(), vecT_B.r()], [T.r()], scale=cw(1), bias=cb)
                    stt(T[:, 1:1024], gp[:, 0:1023], cw(0), T[:, 1:1024], ALU.mult, ALU.add,
                        [gp.r(), T.r(), vecT_C.r()], [T.r()])
                    stt(T[:, 0:1023], gp[:, 1:1024], cw(2), T[:, 0:1023], ALU.mult, ALU.add,
                        [gp.r(), T.r(), vecT_C.r()], [T.r()])
                    if half == 0:
                        stt(T[:, 1023:1024], halo[:, hc:hc + 1], cw(2), T[:, 1023:1024], ALU.mult, ALU.add,
                            [halo.r(hc, hc + 1), T.r(), vecT_C.r()], [T.r()])
                    else:
                        stt(T[:, 0:1], halo[:, hc:hc + 1], cw(0), T[:, 0:1], ALU.mult, ALU.add,
                            [halo.r(hc, hc + 1), T.r(), vecT_C.r()], [T.r()])
                    act(S[:], T[:], AF.Silu, [T.r()], [S.r()])
                    for tb in range(2):
                        tt("dve", aT[:, j, tb * 512:(tb + 1) * 512], S[:, tb * 512:(tb + 1) * 512], vps[tb][:, 0:512], ALU.mult,
                           [S.r(), vps[tb].r()], [aT.r(j * 1024 + tb * 512, j * 1024 + (tb + 1) * 512)])
                for i in range(8):
                    t = half * 8 + i
                    yb = pbank((i % 3) * 2, 2)
                    for j in range(NJ):
                        for dh in range(2):
                            mm(yb[:, dh * 512:(dh + 1) * 512], aT[:, j, i * 128:(i + 1) * 128], wd[:, j, dh * 512:(dh + 1) * 512],
                               j == 0, j == NJ - 1, [aT.r(j * 1024 + i * 128, j * 1024 + (i + 1) * 128), wd.r(j * D, (j + 1) * D)],
                               [yb.r(dh * 512, (dh + 1) * 512)], j == NJ - 1)
                    post_tile(yb, t, ygF[i % 2], final)

        if stop_after == "attn0":
            finish()
            return nc
        ffn(0, False)
        if stop_after == "ffn0":
            finish()
            return nc

        xnT = ar(0, [NT, D], BF16)
        pT = ar(32768, [8, L], BF16)
        poolM = ar(65536, [20, 128], BF16)
        wpl = ar(70656, [8, 256], BF16)
        psbc = ar(74752, [D], F32)
        brow = ar(78848, [D], BF16, parts=1)
        browf = ar(80896, [D], F32, parts=1)
        ygP = [ar(84992 + s * 4096, [D], F32) for s in range(2)]
        junkP = ar(93184, [D], BF16)

        P.dma("pool", "poolM", lambda e: e.dma_start(out=poolM[:], in_=poolM_d.rearrange("b k n -> k b n")),
              writes=[poolM.r()])
        P.dma("pool", "wpl", lambda e: e.dma_start(out=wpl[:], in_=wpool_d.rearrange("g (k p) e -> p (g k) e", p=128)),
              writes=[wpl.r()])
        P.dma("sp", "browf", lambda e: e.dma_start(out=browf[:], in_=bpool_d[:, :]), writes=[browf.r()])
        Vr2 = ar(95232, [D], F32)
        memset("dve", Vr2[:], 0.0, [Vr2.r()])
        P.dma("sp", "vrows2", lambda e: e.dma_start(out=Vr2[0:4, :], in_=vrows_d[:, :]), writes=[Vr2.r()])
        for half in range(2):
            pb = pbank(6 + half)
            bcast_row(Vr2, 2, half * 512, 512, pb)
            cp("act", psbc[:, half * 512:(half + 1) * 512], pb[:, 0:512], [pb.r()], [psbc.r(half * 512, (half + 1) * 512)])
        tt("dve", wpl[:].rearrange("p (g k) e -> p g k e", g=4), wpl[:].rearrange("p (g k) e -> p g k e", g=4),
           psbc[:].rearrange("p (g e) -> p g e", g=4).unsqueeze(2).to_broadcast([128, 4, 2, 256]), ALU.mult,
           [wpl.r(), psbc.r()], [wpl.r()])
        tt("dve", brow[:], browf[:], psbc[0:1, :], ALU.mult, [browf.r(), psbc.r()], [brow.r()])

        for t in range(NT):
            c = stat_cols(1)
            ssr = stat.r(c, c + 1)
            act(junkP[:], xs[:, t, :], AF.Square, [xr(t)], [junkP.r(), ssr], accum_out=stat[:, c:c + 1])
            rstd_from_ss(stat[:, c:c + 1], ssr, 1.0 / D)
            act(xnT[:, t, :], xs[:, t, :], AF.Copy, [xr(t), ssr], [xnT.r(t * D, (t + 1) * D)], scale=stat[:, c:c + 1])
        nb = 0
        for cc in range(8):
            g = cc // 2
            for Tq in range(4):
                pb = pbank(nb % 4)
                nb += 1
                for ti in range(4):
                    T_ = Tq * 4 + ti
                    terms = []
                    if T_ > 0:
                        terms.append((T_ - 1, 3))
                    terms.append((T_, 0 if T_ == 0 else (2 if T_ == NT - 1 else 1)))
                    if T_ < NT - 1:
                        terms.append((T_ + 1, 4))
                    for n_, (st, bi) in enumerate(terms):
                        mm(pb[:, ti * 128:(ti + 1) * 128], xnT[:, st, cc * 128:(cc + 1) * 128], poolM[:, g * 5 + bi, :],
                           n_ == 0, n_ == len(terms) - 1, [xnT.r(st * D, (st + 1) * D), poolM.r()],
                           [pb.r(ti * 128, (ti + 1) * 128)], n_ == len(terms) - 1)
                dst = pT[:, cc, Tq * 512:(Tq + 1) * 512]
                drng = pT.r(cc * L + Tq * 512, cc * L + (Tq + 1) * 512)
                if nb % 2 == 0:
                    act(dst, pb[:, 0:512], AF.Copy, [pb.r(), AmT.r()], [drng], scale=AmT[:, 1, cc:cc + 1])
                else:
                    ts("dve", dst, pb[:, 0:512], AmT[:, 1, cc:cc + 1], None, ALU.mult, ALU.bypass, [pb.r(), AmT.r()], [drng])
        load_gbc(2)
        for t in range(NT):
            yb = pbank(4 + 2 * (t % 2), 2)
            for g in range(4):
                o = yb[:, g * 256:(g + 1) * 256]
                for kk in range(2):
                    mm(o, pT[:, 2 * g + kk, t * 128:(t + 1) * 128], wpl[:, 2 * g + kk, :], kk == 0, False,
                       [pT.r((2 * g + kk) * L + t * 128, (2 * g + kk) * L + (t + 1) * 128), wpl.r()],
                       [yb.r(g * 256, (g + 1) * 256)], False)
                mm(o, onesb[0:1, 0:128], brow[0:1, g * 256:(g + 1) * 256], False, True,
                   [onesb.r(), brow.r()], [yb.r(g * 256, (g + 1) * 256)], True)
            post_tile(yb, t, ygP[t % 2], False)
        if stop_after == "pool1":
            finish()
            return nc
        ffn(1, True)
        finish()
    return nc


def _const_tables():
    tok = np.arange(L)
    row = (tok // 64).astype(np.float64)
    col = (tok % 64).astype(np.float64)
    inv = 10000.0 ** (-np.arange(32, dtype=np.float64) / 32.0)
    inv = inv.astype(np.float32).astype(np.float64)
    ang = np.stack([row[:, None] * inv[None, :], col[:, None] * inv[None, :]], axis=1)
    ang = ang.astype(np.float32)
    cos = np.cos(ang)
    sin = np.sin(ang)
    cosf = np.stack([cos, cos], axis=2).reshape(L, 128)
    sinf = np.stack([-sin, sin], axis=2).reshape(L, 128)
    cos_t = np.ascontiguousarray(cosf.reshape(NT, 128, 128).transpose(1, 0, 2)).astype(np.float32)
    sin_t = np.ascontiguousarray(sinf.reshape(NT, 128, 128).transpose(1, 0, 2)).astype(np.float32)
    blocks = []
    t = np.arange(L)
    for w in (2, 4, 8, 16):
        lo = np.clip(t - w // 2, 0, L)
        hi = np.clip(t + (w - w // 2), 0, L)
        cnt = (hi - lo).astype(np.float32)
        s = np.arange(384)[:, None]
        def blk(s0, t0):
            ss = s0 + np.arange(128)[:, None]
            tt_ = t0 + np.arange(128)[None, :]
            m = ((ss >= lo[tt_]) & (ss < hi[tt_])).astype(np.float32) / cnt[tt_]
            m = m - (ss == tt_).astype(np.float32)
            return m.astype(np.float32)
        blocks += [blk(0, 0), blk(128, 128), blk(1920, 1920), blk(0, 128), blk(256, 128)]
    poolM = np.stack(blocks, axis=0).astype(np.float32)
    return cos_t, sin_t, poolM


_CACHE = {}


def _get_prog(stop_after="full"):
    if stop_after not in _CACHE:
        _CACHE[stop_after] = build_program(stop_after)
    return _CACHE[stop_after]


def make_in_maps(x, c, ctx, c_ctx, w_mod, b_mod, g_pre_mix, g_post_mix, g_pre_ffn, g_post_ffn,
                 w_qkv, g_q, g_k, w_o, w_pool, b_pool, pool_scale, w_up, conv_w, conv_b, w_down):
    f = lambda a: np.ascontiguousarray(np.asarray(a, dtype=np.float32))
    cos_t, sin_t, poolM = _const_tables()
    vecB = np.concatenate([f(g_pre_mix).reshape(16, 128), f(g_pre_ffn).reshape(16, 128),
                           f(conv_b).reshape(44, 128)], axis=0)
    vecC = f(conv_w).reshape(132, 128)
    gpost = np.stack([f(g_post_mix)[0], f(g_post_ffn)[0], f(g_post_mix)[1], f(g_post_ffn)[1]], axis=0)
    vrows = np.zeros((4, D), np.float32)
    vrows[0, :128] = f(g_q)[0]
    vrows[1, :128] = f(g_k)[0]
    vrows[2, :] = f(pool_scale)[0]
    shared = dict(w_mod=f(w_mod), b_mod=f(b_mod), vecB=f(vecB), vecC=vecC, gpost=f(gpost), vrows=vrows,
                  bpool=f(b_pool).reshape(1, D), w_qkv=f(w_qkv)[0], w_o=f(w_o)[0], w_pool=f(w_pool)[0],
                  w_up=f(w_up), w_down=f(w_down), rope_cos=cos_t, rope_sin=sin_t, poolM=poolM)
    x = f(x)
    ctx = f(ctx)
    c = f(c)
    c_ctx = f(c_ctx)
    in_maps = []
    for b in range(8):
        m = dict(shared)
        m["x"] = x[b]
        m["ctx"] = ctx[b]
        m["cc"] = np.concatenate([c[b].reshape(8, 128), c_ctx.reshape(8, 128)], axis=0)
        in_maps.append(m)
    return in_maps


def kernel(**inputs):
    nc = _get_prog("full")
    in_maps = make_in_maps(**inputs)
    res = run_bass_kernel_spmd(nc, in_maps, core_ids=list(range(8)))
    return np.stack([np.asarray(r["y"], dtype=np.float32) for r in res.results], axis=0)
```
